# Optimizing a Trainium2 kernel written in Bass

```python
import math
import jax, jax.numpy as jnp
from jax import lax
import numpy as np

D_MODEL = 2048
BATCH = 8
SEQ = 2048
DEPTH = 4

CHUNK = 64
MIX_WIDTH = D_MODEL
GLA_WIDTH = D_MODEL // 2
GLA_HEADS = 4
GLA_KEY_WIDTH = GLA_WIDTH // 2
GLA_DK = GLA_KEY_WIDTH // GLA_HEADS
GLA_DV = GLA_WIDTH // GLA_HEADS
GLA_GATE_RANK = 16
GLA_GATE_TAU = 16.0
CONV_WIDTH = MIX_WIDTH - GLA_WIDTH
CONV_K = 3
IN_PROJ_WIDTH = 2 * GLA_KEY_WIDTH + 2 * GLA_WIDTH + GLA_GATE_RANK + 3 * CONV_WIDTH
PEER_HEADS = 8
N_KEYS = 128
N_EXPERTS = N_KEYS * N_KEYS
PEER_TOPK = 16
PEER_QDIM = 256
PEER_HALF = PEER_QDIM // 2
PEER_TOKEN_BLOCK = 128
DEEPNORM_ALPHA = (2.0 * DEPTH) ** 0.25
DEEPNORM_BETA = (8.0 * DEPTH) ** -0.25
EPS = 1e-5

kernel_name = "hybrid_gla_shortconv_peer_deepnorm_adaln"


def layer_norm(x, g, b):
    xf = x.astype(jnp.float32)
    mu = jnp.mean(xf, axis=-1, keepdims=True)
    var = jnp.mean(jnp.square(xf - mu), axis=-1, keepdims=True)
    y = (xf - mu) * lax.rsqrt(var + EPS) * g.astype(jnp.float32) + b.astype(jnp.float32)
    return y.astype(x.dtype)


def rms_norm(x, g):
    xf = x.astype(jnp.float32)
    y = xf * lax.rsqrt(jnp.mean(jnp.square(xf), axis=-1, keepdims=True) + EPS) * g.astype(jnp.float32)
    return y.astype(x.dtype)


def gla_chunk_state(q, k, v, a_lr, w_gate2, b_gate):
    bsz, seq, _ = q.shape
    nc = seq // CHUNK
    f32 = jnp.float32
    qf = q.astype(f32).reshape(bsz, nc, CHUNK, GLA_HEADS, GLA_DK) * (GLA_DK ** -0.5)
    kf = k.astype(f32).reshape(bsz, nc, CHUNK, GLA_HEADS, GLA_DK)
    vf = v.astype(f32).reshape(bsz, nc, CHUNK, GLA_HEADS, GLA_DV)
    gate_logit = jnp.einsum('bsr,rk->bsk', a_lr.astype(f32), w_gate2.astype(f32)) + b_gate.astype(f32)
    log_a = (jax.nn.log_sigmoid(gate_logit) / GLA_GATE_TAU).reshape(bsz, nc, CHUNK, GLA_HEADS, GLA_DK)
    cum = jnp.cumsum(log_a, axis=2)
    total = cum[:, :, -1]
    k_dec = kf * jnp.exp(total[:, :, None] - cum)

    def step(state, xs):
        dec, kc, vc, qc = xs
        state = jnp.exp(dec)[..., None] * state + jnp.einsum('blhk,blhv->bhkv', kc, vc)
        return state, jnp.einsum('blhk,bhkv->blhv', qc, state)

    xs = (jnp.moveaxis(total, 1, 0), jnp.moveaxis(k_dec, 1, 0),
          jnp.moveaxis(vf, 1, 0), jnp.moveaxis(qf, 1, 0))
    init = jnp.zeros((bsz, GLA_HEADS, GLA_DK, GLA_DV), f32)
    _, o = lax.scan(step, init, xs)
    return jnp.moveaxis(o, 0, 1).reshape(bsz, seq, GLA_HEADS, GLA_DV)


def token_mixer(h, w_in, w_gate2, b_gate, gla_norm_g, conv_w, conv_norm_g, w_out):
    bsz, seq, _ = h.shape
    proj = jnp.einsum('bsd,de->bse', h, w_in)
    widths = [GLA_KEY_WIDTH, GLA_KEY_WIDTH, GLA_WIDTH, GLA_WIDTH, GLA_GATE_RANK,
              CONV_WIDTH, CONV_WIDTH, CONV_WIDTH]
    split_at = np.cumsum(widths)[:-1].tolist()
    q, k, v, r, a_lr, cb, cc, ch = jnp.split(proj, split_at, axis=-1)

    o = gla_chunk_state(q, k, v, a_lr, w_gate2, b_gate).astype(h.dtype)
    o = rms_norm(o, gla_norm_g) * jax.nn.silu(r.reshape(bsz, seq, GLA_HEADS, GLA_DV))
    y_gla = o.reshape(bsz, seq, GLA_WIDTH)

    u = cc * ch
    up = jnp.pad(u, ((0, 0), (CONV_K - 1, 0), (0, 0)))
    conv = conv_w[0] * up[:, :-2] + conv_w[1] * up[:, 1:-1] + conv_w[2] * up[:, 2:]
    y_conv = rms_norm(cb * conv, conv_norm_g)

    y = jnp.concatenate([y_gla, y_conv], axis=-1)
    return jnp.einsum('bse,ed->bsd', y, w_out)


def peer_ffn(h, wq, keys, u_tab, v_tab):
    bsz, seq, dm = h.shape
    t = bsz * seq
    xt = h.reshape(t, dm)
    q = jnp.einsum('td,de->te', xt, wq).reshape(t, PEER_HEADS, 2, PEER_HALF)
    s = jnp.einsum('thpd,hpnd->thpn', q, keys).astype(jnp.float32)
    sv, si = lax.top_k(s, PEER_TOPK)
    cand = sv[:, :, 0, :, None] + sv[:, :, 1, None, :]
    cv, ci = lax.top_k(cand.reshape(t, PEER_HEADS, PEER_TOPK * PEER_TOPK), PEER_TOPK)
    e1 = jnp.take_along_axis(si[:, :, 0], ci // PEER_TOPK, axis=-1)
    e2 = jnp.take_along_axis(si[:, :, 1], ci % PEER_TOPK, axis=-1)
    experts = e1 * N_KEYS + e2
    gates = jax.nn.softmax(cv, axis=-1).astype(h.dtype)

    nb = t // PEER_TOKEN_BLOCK

    def block(args):
        xb, eb, gb = args
        a = jnp.einsum('thkd,td->thk', u_tab[eb], xb)
        return jnp.einsum('thk,thkd->td', gb * jax.nn.gelu(a), v_tab[eb])

    y = lax.map(block, (xt.reshape(nb, PEER_TOKEN_BLOCK, dm),
                        experts.reshape(nb, PEER_TOKEN_BLOCK, PEER_HEADS, PEER_TOPK),
                        gates.reshape(nb, PEER_TOKEN_BLOCK, PEER_HEADS, PEER_TOPK)))
    return y.reshape(bsz, seq, dm)


def setup_inputs(seed: int = 0) -> dict:
    key = jax.random.key(seed)
    ks = jax.random.split(key, 20)
    nrm = lambda k, shape, s: jax.random.normal(k, shape, jnp.float32) * s
    L, D = DEPTH, D_MODEL
    return {
        "x": nrm(ks[0], (BATCH, SEQ, D), 1.0),
        "c": nrm(ks[1], (BATCH, D), 1.0),
        "ada_w": nrm(ks[2], (L, D, 6 * D), 0.2 * D ** -0.5),
        "ada_b": nrm(ks[3], (L, 6 * D), 0.01),
        "w_in": nrm(ks[4], (L, D, IN_PROJ_WIDTH), D ** -0.5),
        "w_gate2": nrm(ks[5], (L, GLA_GATE_RANK, GLA_KEY_WIDTH), GLA_GATE_RANK ** -0.5),
        "b_gate": nrm(ks[6], (L, GLA_KEY_WIDTH), 0.1),
        "gla_norm_g": 1.0 + nrm(ks[7], (L, GLA_DV), 0.02),
        "conv_w": nrm(ks[8], (L, CONV_K, CONV_WIDTH), CONV_K ** -0.5),
        "conv_norm_g": 1.0 + nrm(ks[9], (L, CONV_WIDTH), 0.02),
        "w_out": nrm(ks[10], (L, MIX_WIDTH, D), DEEPNORM_BETA * MIX_WIDTH ** -0.5),
        "ln1_g": 1.0 + nrm(ks[11], (L, D), 0.02),
        "ln1_b": nrm(ks[12], (L, D), 0.02),
        "peer_wq": nrm(ks[13], (L, D, PEER_HEADS * PEER_QDIM), D ** -0.5),
        "peer_keys": nrm(ks[14], (L, PEER_HEADS, 2, N_KEYS, PEER_HALF), PEER_HALF ** -0.5),
        "peer_u": nrm(ks[15], (L, N_EXPERTS, D), D ** -0.5),
        "peer_v": nrm(ks[16], (L, N_EXPERTS, D), DEEPNORM_BETA * PEER_HEADS ** -0.5),
        "ln2_g": 1.0 + nrm(ks[17], (L, D), 0.02),
        "ln2_b": nrm(ks[18], (L, D), 0.02),
    }


def reference(x, c, ada_w, ada_b, w_in, w_gate2, b_gate, gla_norm_g, conv_w, conv_norm_g,
              w_out, ln1_g, ln1_b, peer_wq, peer_keys, peer_u, peer_v, ln2_g, ln2_b):
    c_act = jax.nn.silu(c)
    for l in range(DEPTH):
        mod = jnp.einsum('bd,de->be', c_act, ada_w[l]) + ada_b[l]
        sh1, sc1, g1, sh2, sc2, g2 = jnp.split(mod[:, None, :], 6, axis=-1)
        h = x * (1.0 + sc1) + sh1
        mix = token_mixer(h, w_in[l], w_gate2[l], b_gate[l], gla_norm_g[l], conv_w[l],
                          conv_norm_g[l], w_out[l])
        x = layer_norm(DEEPNORM_ALPHA * x + (1.0 + g1) * mix, ln1_g[l], ln1_b[l])
        h = x * (1.0 + sc2) + sh2
        ffn = peer_ffn(h, peer_wq[l], peer_keys[l], peer_u[l], peer_v[l])
        x = layer_norm(DEEPNORM_ALPHA * x + (1.0 + g2) * ffn, ln2_g[l], ln2_b[l])
    return x
```

```python
import math
from contextlib import ExitStack
import numpy as np
import concourse.bass as bass
import concourse.mybir as mybir
from concourse.bass_utils import run_bass_kernel_spmd

F32 = mybir.dt.float32
BF16 = mybir.dt.bfloat16
AF = mybir.ActivationFunctionType
ALU = mybir.AluOpType
AX = mybir.AxisListType

NCORES = 8
DEBUG = False
STAGE = 9
PCUT = 9
D = 2048
KC = D // 128
DEPTH_FULL = 4
ALPHA = (2.0 * DEPTH_FULL) ** 0.25
EPS = 1e-5
INW = 6160
NEXP = 16384
TAU = 16.0
C_GELU = math.sqrt(2.0 / math.pi)
OQ, OK_, OV, OR, OA, OCB, OCC, OCH = 0, 512, 1024, 2048, 3072, 3088, 4112, 5136


class TB:
    __slots__ = ("w", "r", "name")

    def __init__(self, name=""):
        self.w = []
        self.r = []
        self.name = name


class Sched:
    ENG = ("pe", "act", "dve", "pool", "sp")
    EPOCH = 30000

    def __init__(self, nc, stack):
        self.nc = nc
        self.stack = stack
        self.ops = {e: [] for e in self.ENG}
        self.cnt = {e: 0 for e in self.ENG}
        self.epoch = {e: 0 for e in self.ENG}
        self.seen = {e: {} for e in self.ENG}
        self.sems = {}
        self.dcnt = {}
        self.depoch = {}
        self.nsem = 0

    def sem(self, key):
        if key not in self.sems:
            self.nsem += 1
            self.sems[key] = self.stack.enter_context(self.nc.semaphore("s%d" % self.nsem))
        return self.sems[key]

    def _waits(self, eng, reads, writes):
        evs = {}
        for b in reads:
            for (k, v) in b.w:
                if evs.get(k, 0) < v:
                    evs[k] = v
        for b in writes:
            for (k, v) in b.w:
                if evs.get(k, 0) < v:
                    evs[k] = v
            for (k, v) in b.r:
                if evs.get(k, 0) < v:
                    evs[k] = v
        out = []
        seen = self.seen[eng]
        for k, v in evs.items():
            if eng == "pe" and k[0] == "e" and k[1] == "pe":
                continue
            if seen.get(k, 0) < v:
                seen[k] = v
                out.append((k, v))
        return out

    def add(self, eng, fn, reads=(), writes=()):
        waits = self._waits(eng, reads, writes)
        if self.cnt[eng] >= self.EPOCH:
            self.epoch[eng] += 1
            self.cnt[eng] = 0
        key = ("e", eng, self.epoch[eng])
        self.sem(key)
        self.cnt[eng] += 1
        ev = (key, self.cnt[eng])
        self.ops[eng].append((waits, fn, key, 1))
        for b in reads:
            b.r.append(ev)
        for b in writes:
            b.w = [ev]
            b.r = []
        return ev

    def dma(self, eng, fn, semkey, reads=(), writes=()):
        ep = self.depoch.get(semkey, 0)
        if self.dcnt.get(("d", semkey, ep), 0) + 16 > self.EPOCH:
            ep += 1
            self.depoch[semkey] = ep
        key = ("d", semkey, ep)
        waits = [w for w in self._waits(eng, reads, writes) if w[0] != key]
        self.sem(key)
        self.dcnt[key] = self.dcnt.get(key, 0) + 16
        ev = (key, self.dcnt[key])
        self.ops[eng].append((waits, fn, key, 16))
        for b in reads:
            b.r.append(ev)
        for b in writes:
            b.w = [ev]
            b.r = []
        return ev

    def raw(self, eng, fn):
        self.ops[eng].append(([], fn, None, 0))

    def replay(self, eng, e):
        for (waits, fn, key, inc) in self.ops[eng]:
            for (k, v) in waits:
                e.wait_ge(self.sems[k], v)
            ins = fn(e)
            if key is not None:
                ins.then_inc(self.sems[key], inc)


def build_program(T, L):
    NT = T // 128
    nc = bass.Bass("TRN2", target_bir_lowering=False)
    st = ExitStack()
    S = Sched(nc, st)

    def din(name, shape, dt=F32):
        return nc.dram_tensor(name, list(shape), dt, kind="ExternalInput").ap()

    x_in = din("x", [T, D])
    c_col = din("c_col", [128, KC])
    adaw_s = din("adaw_s", [L * 2 * 128, 6 * D])
    win_s = din("win_s", [L * 256, INW])
    woq_s = din("woq_s", [L * 512, D])
    ut_s = din("ut_s", [L * 4 * 512, D])
    v_s = din("v_s", [L * 2 * 1024, D])
    ada_b = din("ada_b", [L, 6 * D])
    wg_in = din("wg", [L, 32, 512])
    gg_in = din("gg", [L, 256])
    cw_in = din("cw", [L, 128, 24])
    cg_in = din("cg", [L, 128, 8])
    ln1g = din("ln1g", [L, D])
    ln1b = din("ln1b", [L, D])
    ln2g = din("ln2g", [L, D])
    ln2b = din("ln2b", [L, D])
    keys_in = din("keysT", [L, 128, 16 * 128])
    consts = din("consts", [128, 5 * 128])
    out_d = nc.dram_tensor("out", [T, D], F32, kind="ExternalOutput").ap()

    def dint(name, shape, dt):
        return nc.dram_tensor(name, list(shape), dt)

    adaw_f = dint("adaw_f", [L * D, 6 * D], BF16)
    win_f = dint("win_f", [L * D, INW], BF16)
    woq_f = dint("woq_f", [L * 2 * D, D], BF16)
    ut_f = dint("ut_f", [L * D * 8, D], BF16)
    v_f = dint("v_f", [L * NEXP, D], BF16)
    mod_d = dint("mod_d", [L, 6 * D], F32)
    if DEBUG:
        xs = [nc.dram_tensor("xs0", [T, D], F32, kind="ExternalOutput"), dint("xs1", [T, D], F32)]
    else:
        xs = [dint("xs0", [T, D], F32), dint("xs1", [T, D], F32)]
    WTB = TB("weights")
    MODTB = TB("mod_d")
    xs_tb = [[TB() for _ in range(NT)] for _ in range(2)]
    xin_tb = TB("xin")
    out_tb = TB("out")

    gathers = []
    def plan(shard, full, rows_full_per_layer, ngroups, W, name):
        GR = rows_full_per_layer // ngroups
        SR = GR // NCORES
        for l in range(L):
            for g in range(ngroups):
                i = l * ngroups + g
                stage = dint("st_%s_%d" % (name, i), [SR, W], BF16)
                gathers.append((shard[i * SR:(i + 1) * SR, :], stage,
                                full.ap()[l * rows_full_per_layer + g * GR: l * rows_full_per_layer + (g + 1) * GR, :]))
    plan(adaw_s, adaw_f, D, 4, 6 * D, "adaw")
    plan(win_s, win_f, D, 2, INW, "win")
    plan(woq_s, woq_f, 2 * D, 1, D, "woq")
    plan(ut_s, ut_f, 8 * D, 4, D, "ut")
    plan(v_s, v_f, NEXP, 4, D, "v")
    cast_sem = st.enter_context(nc.semaphore("castsem"))
    cc_sem = st.enter_context(nc.semaphore("ccsem"))
    NG = len(gathers)

    def emit_gathers(g):
        for i, (sh, stg, outap) in enumerate(gathers):
            g.dma_start(out=stg.ap(), in_=sh, max_dma_last_dim=4096).then_inc(cast_sem, 16)
            g.wait_ge(cast_sem, 16 * (i + 1))
            g.collective_compute("AllGather", ALU.bypass, replica_groups=[list(range(NCORES))],
                                 ins=[stg.ap().opt()], outs=[outap.opt()]).then_inc(cc_sem)
        g.wait_ge(cc_sem, NG)
    S.sems[("cc",)] = cc_sem
    WTB.w = [(("cc",), NG)]
    S.seen["pool"][("cc",)] = NG

    def sb(name, shape, dt=F32):
        return st.enter_context(nc.sbuf_tensor(name, list(shape), dt))

    def ps(name, shape, dt=F32):
        return st.enter_context(nc.psum_tensor(name, list(shape), dt))

    cst = sb("cst", [128, 5 * 128]); cst_tb = TB()
    ident_f = cst[:, 0:128]
    mstrict = cst[:, 128:256]
    ones_f = cst[:, 256:384]
    cmask = cst[:, 384:386]
    m01 = cst[:, 386:388]
    ident_b_t = sb("ident_b", [128, 128], BF16); identb_tb = TB()
    ident_b = ident_b_t[:, :]
    WB = [sb("wb%d" % i, [128, 8192], BF16) for i in range(3)]
    WB_tb = [TB() for _ in range(3)]
    FB = [sb("fb%d" % i, [128, 2048], F32) for i in range(11)]
    FB_tb = [TB() for _ in range(11)]
    hT = sb("hT", [128, KC, 128], BF16); hT_tb = TB()
    vb = sb("vb", [128, 1024], BF16); vb_tb = TB()
    qm = sb("qm", [128, 4, 2, 128], BF16); qm_tb = TB()
    kd = sb("kd", [128, 2, 512], BF16); kd_tb = TB()
    Sb = sb("Sb", [128, 2, 4, 256], BF16); Sb_tb = [[TB() for _ in range(4)] for _ in range(2)]
    yg = sb("yg", [128, 1024], BF16); yg_tb = TB()
    yT = sb("yT", [128, KC, 128], BF16); yTg_tb = TB(); yTc_tb = TB()
    Wb = [sb("Wb%d" % i, [128, 512], BF16) for i in range(2)]; Wb_tb = [TB(), TB()]
    WTt = [sb("WT%d" % i, [128, 4, 128], BF16) for i in range(2)]; WT_tb = [TB(), TB()]
    Sst = sb("Sst", [128, 4, 256]); Sst_tb = [TB() for _ in range(4)]
    ub = sb("ub", [128, 8, 130]); ub_tb = TB()
    svt = sb("sv", [128, 16, 16]); sv_tb = TB()
    cvv = sb("cvv", [128, 8, 16]); cvv_tb = TB()
    s0p = sb("s0p", [128, 8, 128]); s0p_tb = TB()
    gt = [sb("gt%d" % i, [128, 512]) for i in range(2)]; gt_tb = [TB(), TB()]
    modrows = sb("modrows", [96, 128]); modrows_tb = TB()
    modcols = sb("modcols", [128, 96]); modcols_tb = TB()
    cact = sb("cact", [128, KC], BF16); cact_tb = TB()
    ccol = sb("ccol", [128, KC]); ccol_tb = TB()
    ggb = sb("ggb", [128, 256]); ggb_tb = TB()
    wgt = sb("wgt", [32, 512]); wgt_tb = TB()
    alrT = sb("alrT", [32, 128]); alrT_tb = TB()
    cwt = sb("cwt", [128, 24]); cwt_tb = TB()
    cgt = sb("cgt", [128, 8]); cgt_tb = TB()
    decx = sb("decx", [128, 8]); decx_tb = TB()
    sm = sb("sm", [128, 64]); sm_tb = TB()
    rstdc = sb("rstdc", [128, 128]); rstdc_tb = TB()
    bnst = sb("bnst", [128, 4, 6]); bnst_tb = TB()
    mv = sb("mv", [128, 2]); mv_tb = TB()
    abrow = sb("abrow", [1, 2048]); abrow_tb = TB()
    mrow = sb("mrow", [1, 2048]); mrow_tb = TB()

    pbig = ps("pbig", [128, 2048]); pb_tb = [TB() for _ in range(4)]
    pA = [ps("pA%d" % i, [128, 512]) for i in range(2)]; pA_tb = [TB(), TB()]
    pF = ps("pF", [128, 512]); pF_tb = TB()
    pT = ps("pT", [128, 1024], BF16); _ptb = TB(); pT_tb = [_ptb, _ptb]

    def pbk(n):
        return pbig[:, n * 512:(n + 1) * 512]

    S.dma("sp", lambda e: e.dma_start(out=cst[:, :], in_=consts), "cst", writes=[cst_tb])
    S.add("act", lambda e: e.activation(out=ident_b, in_=ident_f, func=AF.Copy), reads=[cst_tb], writes=[identb_tb])
    S.add("dve", lambda e: e.memset(qm[:, :, :, :], 0.0), writes=[qm_tb])
    S.add("dve", lambda e: e.memset(alrT[:, :], 1.0), writes=[alrT_tb])
    S.dma("sp", lambda e: e.dma_start(out=ccol[:, :], in_=c_col), "ccol", writes=[ccol_tb])
    S.add("act", lambda e: e.activation(out=cact[:, :], in_=ccol[:, :], func=AF.Silu), reads=[ccol_tb], writes=[cact_tb])

    wrot = [0]

    def wload(src_ap, view, extra_reads=()):
        i = wrot[0] % 3
        wrot[0] += 1
        dst = view(WB[i])
        if len(dst.shape) == 3 and dst.shape[1] > 2:
            kk = 2
            for k0 in range(0, dst.shape[1], kk):
                S.dma("sp", lambda e, d=dst[:, k0:k0 + kk, :], s=src_ap[:, k0:k0 + kk, :]: e.dma_start(out=d, in_=s), "wb%d" % i,
                      reads=[WTB] + list(extra_reads), writes=[WB_tb[i]])
        else:
            S.dma("sp", lambda e, d=dst, s=src_ap: e.dma_start(out=d, in_=s), "wb%d" % i,
                  reads=[WTB] + list(extra_reads), writes=[WB_tb[i]])
        return dst, WB_tb[i]

    v16x512 = lambda t: t[:, :].rearrange("p (k c) -> p k c", k=16)
    v4x2048 = lambda t: t[:, :].rearrange("p (k c) -> p k c", k=4)

    for l in range(L if STAGE >= 1 else 0):
        for j in range(6):
            S.dma("sp", lambda e, l=l, j=j: e.dma_start(out=abrow[:, :], in_=ada_b[l:l + 1, j * D:(j + 1) * D]), "abrow",
                  writes=[abrow_tb])
            for k in range(KC):
                src = adaw_f.ap()[l * D + k * 128: l * D + (k + 1) * 128, j * D:(j + 1) * D]
                wv, wtb = wload(src, lambda t: t[:, 0:2048])
                for n in range(4):
                    S.add("pe", lambda e, n=n, k=k, wv=wv: e.matmul(out=pbig[0:1, n * 512:(n + 1) * 512], lhsT=cact[:, k:k + 1],
                                                                 rhs=wv[:, n * 512:(n + 1) * 512], start=(k == 0), stop=(k == KC - 1)),
                          reads=[cact_tb, wtb], writes=[pb_tb[n]])
            addc = 1.0 if j in (1, 2, 4, 5) else 0.0
            for n in range(4):
                S.add("dve", lambda e, n=n, addc=addc: e.scalar_tensor_tensor(
                    out=mrow[:, n * 512:(n + 1) * 512], in0=pbig[0:1, n * 512:(n + 1) * 512], scalar=addc,
                    in1=abrow[:, n * 512:(n + 1) * 512], op0=ALU.add, op1=ALU.add),
                    reads=[pb_tb[n], abrow_tb], writes=[mrow_tb])
            S.dma("sp", lambda e, l=l, j=j: e.dma_start(out=mod_d.ap()[l:l + 1, j * D:(j + 1) * D], in_=mrow[:, :]), "mrow",
                  reads=[mrow_tb], writes=[MODTB])

    XT, G1B, LGB, LBB, TA, TBF = 0, 1, 2, 3, 4, 5

    def load_layer_common(l, which):
        gi = 2 if which == 0 else 5
        lg = ln1g if which == 0 else ln2g
        lb = ln1b if which == 0 else ln2b
        S.dma("sp", lambda e: e.dma_start(out=FB[G1B][:, :], in_=mod_d.ap()[l:l + 1, gi * D:(gi + 1) * D].to_broadcast([128, D])),
              "fb%d" % G1B, reads=[MODTB], writes=[FB_tb[G1B]])
        S.dma("sp", lambda e: e.dma_start(out=FB[LGB][:, :], in_=lg[l:l + 1, :].to_broadcast([128, D])), "fb%d" % LGB,
              writes=[FB_tb[LGB]])
        S.dma("sp", lambda e: e.dma_start(out=FB[LBB][:, :], in_=lb[l:l + 1, :].to_broadcast([128, D])), "fb%d" % LBB,
              writes=[FB_tb[LBB]])

    def load_modcols(l):
        S.dma("sp", lambda e: e.dma_start(out=modrows[:, :], in_=mod_d.ap()[l:l + 1, :].rearrange("o (r p) -> (o r) p", p=128)),
              "modrows", reads=[MODTB], writes=[modrows_tb])
        S.add("pe", lambda e: e.transpose(out=pF[:, 0:96], in_=modrows[:, :], identity=ident_f[0:96, 0:96]),
              reads=[modrows_tb, cst_tb], writes=[pF_tb])
        S.add("act", lambda e: e.activation(out=modcols[:, :], in_=pF[:, 0:96], func=AF.Copy), reads=[pF_tb], writes=[modcols_tb])

    def load_x_and_hT(src_ap, src_tb, i, shj, scj):
        S.dma("sp", lambda e: e.dma_start(out=FB[XT][:, :], in_=src_ap[i * 128:(i + 1) * 128, :]), "fb%d" % XT,
              reads=[src_tb], writes=[FB_tb[XT]])
        for k in range(KC):
            if k % 4 == 0:
                pass
            S.add("pe", lambda e, k=k: e.transpose(out=pF[:, (k % 4) * 128:(k % 4 + 1) * 128], in_=FB[XT][:, k * 128:(k + 1) * 128],
                                                  identity=ident_f), reads=[FB_tb[XT], cst_tb], writes=[pF_tb])
            S.add("act", lambda e, k=k: e.activation(out=hT[:, k, :], in_=pF[:, (k % 4) * 128:(k % 4 + 1) * 128], func=AF.Identity,
                                                    scale=modcols[:, scj * 16 + k: scj * 16 + k + 1],
                                                    bias=modcols[:, shj * 16 + k: shj * 16 + k + 1]),
                  reads=[pF_tb, modcols_tb], writes=[hT_tb])

    def epilogue(l, i, dst_ap, dst_tb):
        for n in range(4):
            S.add("dve", lambda e, n=n: e.tensor_tensor(out=FB[TA][:, n * 512:(n + 1) * 512], in0=pbk(n),
                                                        in1=FB[G1B][:, n * 512:(n + 1) * 512], op=ALU.mult),
                  reads=[pb_tb[n], FB_tb[G1B]], writes=[FB_tb[TA]])
        S.add("dve", lambda e: e.scalar_tensor_tensor(out=FB[TA][:, :], in0=FB[XT][:, :], scalar=ALPHA, in1=FB[TA][:, :],
                                                      op0=ALU.mult, op1=ALU.add),
              reads=[FB_tb[XT], FB_tb[TA]], writes=[FB_tb[TA]])
        for n in range(4):
            S.add("dve", lambda e, n=n: e.bn_stats(out=bnst[:, n, :], in_=FB[TA][:, n * 512:(n + 1) * 512]),
                  reads=[FB_tb[TA]], writes=[bnst_tb])
        S.add("dve", lambda e: e.bn_aggr(out=mv[:, :], in_=bnst[:, :, :].rearrange("p a b -> p (a b)")), reads=[bnst_tb], writes=[mv_tb])
        S.add("dve", lambda e: e.tensor_scalar(out=sm[:, 0:1], in0=mv[:, 1:2], scalar1=EPS, scalar2=None, op0=ALU.add),
              reads=[mv_tb], writes=[sm_tb])
        S.add("act", lambda e: e.activation(out=sm[:, 1:2], in_=sm[:, 0:1], func=AF.Sqrt), reads=[sm_tb], writes=[sm_tb])
        S.add("dve", lambda e: e.reciprocal(out=sm[:, 2:3], in_=sm[:, 1:2]), reads=[sm_tb], writes=[sm_tb])
        S.add("dve", lambda e: e.tensor_scalar(out=FB[TBF][:, :], in0=FB[TA][:, :], scalar1=mv[:, 0:1], scalar2=sm[:, 2:3],
                                               op0=ALU.subtract, op1=ALU.mult),
              reads=[FB_tb[TA], mv_tb, sm_tb], writes=[FB_tb[TBF]])
        S.add("pool", lambda e: e.tensor_tensor(out=FB[TBF][:, :], in0=FB[TBF][:, :], in1=FB[LGB][:, :], op=ALU.mult),
              reads=[FB_tb[TBF], FB_tb[LGB]], writes=[FB_tb[TBF]])
        S.add("pool", lambda e: e.tensor_tensor(out=FB[TBF][:, :], in0=FB[TBF][:, :], in1=FB[LBB][:, :], op=ALU.add),
              reads=[FB_tb[TBF], FB_tb[LBB]], writes=[FB_tb[TBF]])
        S.dma("sp", lambda e: e.dma_start(out=dst_ap[i * 128:(i + 1) * 128, :], in_=FB[TBF][:, :]), "fb%d" % TBF,
              reads=[FB_tb[TBF]], writes=[dst_tb])

    MA, MB, MC, MD, ME = 6, 7, 8, 9, 10
    kf = FB[MA][:, 0:512]; lsp = FB[MA][:, 512:1024]; expD = FB[MA][:, 1024:1536]; osq = FB[MA][:, 1536:1792]
    rs = FB[MB][:, 0:1024]; ytmp = FB[MB][:, 1024:2048]
    ccs = FB[MC][:, 0:1024].rearrange("p (j t) -> p j t", j=8); cbs = FB[MC][:, 1024:2048].rearrange("p (j t) -> p j t", j=8)
    cvt = FB[MD][:, 0:1024].rearrange("p (j t) -> p j t", j=8); zt = FB[MD][:, 1024:2048].rearrange("p (j t) -> p j t", j=8)
    zsq = FB[ME][:, 0:1024].rearrange("p (j t) -> p j t", j=8)
    MA_tb, MB_tb, MC_tb, MD_tb, ME_tb = FB_tb[MA], FB_tb[MB], FB_tb[MC], FB_tb[MD], FB_tb[ME]

    def mixer_layer(l, src_ap, src_tbs, dst_ap, dst_tbs):
        load_modcols(l)
        load_layer_common(l, 0)
        S.dma("sp", lambda e: e.dma_start(out=ggb[:, :], in_=gg_in[l:l + 1, :].to_broadcast([128, 256])), "ggb", writes=[ggb_tb])
        S.dma("sp", lambda e: e.dma_start(out=wgt[:, :], in_=wg_in[l]), "wgt", writes=[wgt_tb])
        S.dma("sp", lambda e: e.dma_start(out=cwt[:, :], in_=cw_in[l]), "cwt", writes=[cwt_tb])
        S.dma("sp", lambda e: e.dma_start(out=cgt[:, :], in_=cg_in[l]), "cgt", writes=[cgt_tb])
        S.add("dve", lambda e: e.memset(Sst[:, :, :], 0.0), writes=Sst_tb)
        S.add("dve", lambda e: e.memset(ub[:, :, :], 0.0), writes=[ub_tb])
        wrow = win_f.ap()[l * D:(l + 1) * D, :]
        worow = woq_f.ap()[l * 2 * D:l * 2 * D + D, :]

        def wchunk(c0, width=512):
            src = wrow[:, c0:c0 + width].rearrange("(k p) c -> p k c", p=128)
            if width == 512:
                return wload(src, v16x512)
            return wload(src, lambda t: t[:, 0:16 * width].rearrange("p (k c) -> p k c", k=16))

        for i in range(NT):
            load_x_and_hT(src_ap, src_tbs[i], i, 0, 1)
            def tok_proj(c0, evac):
                wv, wtb = wchunk(c0)
                pa = 0
                for k in range(KC):
                    S.add("pe", lambda e, k=k, wv=wv: e.matmul(out=pA[pa][:, :], lhsT=hT[:, k, :], rhs=wv[:, k, :],
                                                              start=(k == 0), stop=(k == KC - 1)),
                          reads=[hT_tb, wtb], writes=[pA_tb[pa]])
                evac(pA[pa], pA_tb[pa])
            tok_proj(OK_, lambda p, ptb: S.add("act", lambda e: e.activation(out=kf, in_=p[:, :], func=AF.Copy),
                                                reads=[ptb], writes=[MA_tb]))
            for hh in range(2):
                tok_proj(OV + hh * 512, lambda p, ptb, hh=hh: S.add(
                    "act", lambda e: e.activation(out=vb[:, hh * 512:(hh + 1) * 512], in_=p[:, :], func=AF.Copy),
                    reads=[ptb], writes=[vb_tb]))
            for hh in range(2):
                tok_proj(OR + hh * 512, lambda p, ptb, hh=hh: S.add(
                    "act", lambda e: e.activation(out=rs[:, hh * 512:(hh + 1) * 512], in_=p[:, :], func=AF.Silu),
                    reads=[ptb], writes=[MB_tb]))
            wv, wtb = wchunk(OQ)
            for h in range(4):
                for k in range(KC):
                    S.add("pe", lambda e, h=h, k=k, wv=wv: e.matmul(out=pA[1][:, h * 128:(h + 1) * 128], lhsT=wv[:, k, h * 128:(h + 1) * 128],
                                                                   rhs=hT[:, k, :], start=(k == 0), stop=(k == KC - 1)),
                          reads=[hT_tb, wtb], writes=[pA_tb[1]])
            pq = pA[1][:, :].rearrange("p (h t) -> p h t", h=4)
            S.add("act", lambda e: e.activation(out=qm[:, :, 0, 0:64], in_=pq[:, :, 0:64], func=AF.Copy, scale=128.0 ** -0.5),
                  reads=[pA_tb[1]], writes=[qm_tb])
            S.add("act", lambda e: e.activation(out=qm[:, :, 1, 64:128], in_=pq[:, :, 64:128], func=AF.Copy, scale=128.0 ** -0.5),
                  reads=[pA_tb[1]], writes=[qm_tb])
            wv, wtb = wchunk(OA, 16)
            for k in range(KC):
                S.add("pe", lambda e, k=k, wv=wv: e.matmul(out=pF[0:16, 0:128], lhsT=wv[:, k, :], rhs=hT[:, k, :],
                                                          start=(k == 0), stop=(k == KC - 1)),
                      reads=[hT_tb, wtb], writes=[pF_tb])
            S.add("act", lambda e: e.activation(out=alrT[0:16, :], in_=pF[0:16, 0:128], func=AF.Copy), reads=[pF_tb], writes=[alrT_tb])
            S.add("pe", lambda e: e.matmul(out=pA[0][:, :], lhsT=alrT[:, :], rhs=wgt[:, :], start=True, stop=True),
                  reads=[alrT_tb, wgt_tb], writes=[pA_tb[0]])
            S.add("act", lambda e: e.activation(out=lsp, in_=pA[0][:, :], func=AF.Exp, scale=-1.0), reads=[pA_tb[0]], writes=[MA_tb])
            S.add("act", lambda e: e.activation(out=lsp, in_=lsp, func=AF.Ln, bias=1.0), reads=[MA_tb], writes=[MA_tb])
            S.add("pe", lambda e: e.matmul(out=pA[0][:, :], lhsT=mstrict, rhs=lsp, start=True, stop=True),
                  reads=[cst_tb, MA_tb], writes=[pA_tb[0]])
            S.add("act", lambda e: e.activation(out=expD, in_=pA[0][:, :], func=AF.Exp), reads=[pA_tb[0]], writes=[MA_tb])
            for c in range(2):
                S.add("dve", lambda e, c=c: e.scalar_tensor_tensor(out=kd[:, c, :], in0=kf, scalar=m01[:, c:c + 1], in1=expD,
                                                                   op0=ALU.mult, op1=ALU.mult),
                      reads=[MA_tb, cst_tb], writes=[kd_tb])
            for h in range(4):
                S.add("pe", lambda e, h=h: e.matmul(out=pF[:, 256 + h * 2: 256 + h * 2 + 2], lhsT=lsp[:, h * 128:(h + 1) * 128], rhs=cmask,
                                                   start=True, stop=True), reads=[MA_tb, cst_tb], writes=[pF_tb])
            S.add("act", lambda e: e.activation(out=decx[:, :], in_=pF[:, 256:264], func=AF.Exp), reads=[pF_tb], writes=[decx_tb])
            for c in range(2):
                for h in range(4):
                    S.add("pe", lambda e, c=c, h=h: e.matmul(out=pF[:, 0:256], lhsT=kd[:, c, h * 128:(h + 1) * 128],
                                                            rhs=vb[:, h * 256:(h + 1) * 256], start=True, stop=True),
                          reads=[kd_tb, vb_tb], writes=[pF_tb])
                    S.add("dve", lambda e, c=c, h=h: e.scalar_tensor_tensor(out=Sst[:, h, :], in0=Sst[:, h, :],
                                                                           scalar=decx[:, h * 2 + c: h * 2 + c + 1], in1=pF[:, 0:256],
                                                                           op0=ALU.mult, op1=ALU.add),
                          reads=[Sst_tb[h], decx_tb, pF_tb], writes=[Sst_tb[h]])
                    S.add("act", lambda e, c=c, h=h: e.activation(out=Sb[:, c, h, :], in_=Sst[:, h, :], func=AF.Copy),
                          reads=[Sst_tb[h]], writes=[Sb_tb[c][h]])
            for h in range(4):
                for c in range(2):
                    S.add("pe", lambda e, c=c, h=h: e.matmul(out=pbig[:, h * 256:(h + 1) * 256], lhsT=qm[:, h, c, :], rhs=Sb[:, c, h, :],
                                                            start=(c == 0), stop=(c == 1)),
                          reads=[qm_tb, Sb_tb[c][h]], writes=[pb_tb[h // 2]])
            for h in range(4):
                S.add("act", lambda e, h=h: e.activation(out=osq, in_=pbig[:, h * 256:(h + 1) * 256], func=AF.Square,
                                                        accum_out=sm[:, 8 + h: 9 + h]),
                      reads=[pb_tb[h // 2]], writes=[MA_tb, sm_tb])
            S.add("dve", lambda e: e.tensor_scalar(out=sm[:, 12:16], in0=sm[:, 8:12], scalar1=1.0 / 256.0, scalar2=EPS,
                                                   op0=ALU.mult, op1=ALU.add), reads=[sm_tb], writes=[sm_tb])
            S.add("act", lambda e: e.activation(out=sm[:, 16:20], in_=sm[:, 12:16], func=AF.Sqrt), reads=[sm_tb], writes=[sm_tb])
            S.add("dve", lambda e: e.reciprocal(out=sm[:, 20:24], in_=sm[:, 16:20]), reads=[sm_tb], writes=[sm_tb])
            for h in range(4):
                S.add("dve", lambda e, h=h: e.scalar_tensor_tensor(out=ytmp[:, h * 256:(h + 1) * 256], in0=pbig[:, h * 256:(h + 1) * 256],
                                                                   scalar=sm[:, 20 + h: 21 + h], in1=ggb[:, :],
                                                                   op0=ALU.mult, op1=ALU.mult),
                      reads=[pb_tb[h // 2], sm_tb, ggb_tb], writes=[MB_tb])
            S.add("dve", lambda e: e.tensor_tensor(out=yg[:, :], in0=ytmp, in1=rs, op=ALU.mult), reads=[MB_tb], writes=[yg_tb])
            for j in range(8):
                S.add("pe", lambda e, j=j: e.transpose(out=pT[:, j * 128:(j + 1) * 128], in_=yg[:, j * 128:(j + 1) * 128], identity=ident_b),
                      reads=[yg_tb, identb_tb], writes=[pT_tb[j // 4]])
            for hh in range(2):
                S.add("act", lambda e, hh=hh: e.activation(out=yT[:, hh * 4:(hh + 1) * 4, :],
                                                          in_=pT[:, hh * 512:(hh + 1) * 512].rearrange("p (j t) -> p j t", j=4), func=AF.Copy),
                      reads=[pT_tb[hh]], writes=[yTg_tb])
            def feat_proj(c0, evac):
                for hh in range(2):
                    wv, wtb = wchunk(c0 + hh * 512)
                    pa = hh
                    for jj in range(4):
                        for k in range(KC):
                            S.add("pe", lambda e, jj=jj, k=k, wv=wv, pa=pa: e.matmul(out=pA[pa][:, jj * 128:(jj + 1) * 128],
                                                                                 lhsT=wv[:, k, jj * 128:(jj + 1) * 128], rhs=hT[:, k, :],
                                                                                 start=(k == 0), stop=(k == KC - 1)),
                                  reads=[hT_tb, wtb], writes=[pA_tb[pa]])
                    evac(hh, pA[pa], pA_tb[pa])
            feat_proj(OCB, lambda hh, p, ptb: S.add("act", lambda e: e.activation(
                out=cbs[:, hh * 4:(hh + 1) * 4, :], in_=p[:, :].rearrange("p (j t) -> p j t", j=4), func=AF.Copy), reads=[ptb], writes=[MC_tb]))
            feat_proj(OCC, lambda hh, p, ptb: S.add("act", lambda e: e.activation(
                out=ccs[:, hh * 4:(hh + 1) * 4, :], in_=p[:, :].rearrange("p (j t) -> p j t", j=4), func=AF.Copy), reads=[ptb], writes=[MC_tb]))
            feat_proj(OCH, lambda hh, p, ptb: S.add("dve", lambda e: e.tensor_tensor(
                out=ub[:, hh * 4:(hh + 1) * 4, 2:130], in0=p[:, :].rearrange("p (j t) -> p j t", j=4), in1=ccs[:, hh * 4:(hh + 1) * 4, :],
                op=ALU.mult), reads=[ptb, MC_tb], writes=[ub_tb]))
            for j in range(8):
                S.add("dve", lambda e, j=j: e.tensor_scalar(out=cvt[:, j, :], in0=ub[:, j, 2:130], scalar1=cwt[:, j * 3 + 2: j * 3 + 3],
                                                            scalar2=None, op0=ALU.mult), reads=[ub_tb, cwt_tb], writes=[MD_tb])
                S.add("dve", lambda e, j=j: e.scalar_tensor_tensor(out=cvt[:, j, :], in0=ub[:, j, 1:129], scalar=cwt[:, j * 3 + 1: j * 3 + 2],
                                                                   in1=cvt[:, j, :], op0=ALU.mult, op1=ALU.add),
                      reads=[ub_tb, cwt_tb, MD_tb], writes=[MD_tb])
                S.add("dve", lambda e, j=j: e.scalar_tensor_tensor(out=cvt[:, j, :], in0=ub[:, j, 0:128], scalar=cwt[:, j * 3: j * 3 + 1],
                                                                   in1=cvt[:, j, :], op0=ALU.mult, op1=ALU.add),
                      reads=[ub_tb, cwt_tb, MD_tb], writes=[MD_tb])
            S.add("dve", lambda e: e.tensor_tensor(out=zt, in0=cbs, in1=cvt, op=ALU.mult), reads=[MC_tb, MD_tb], writes=[MD_tb])
            S.add("pool", lambda e: e.tensor_tensor(out=zsq, in0=zt, in1=zt, op=ALU.mult), reads=[MD_tb], writes=[ME_tb])
            S.add("pool", lambda e: e.tensor_copy(out=ub[:, :, 0:2], in_=ub[:, :, 128:130]), reads=[ub_tb, MD_tb], writes=[ub_tb])
            for j in range(8):
                S.add("pe", lambda e, j=j: e.matmul(out=pF[:, 384:512], lhsT=ones_f, rhs=zsq[:, j, :], start=(j == 0), stop=(j == 7)),
                      reads=[cst_tb, ME_tb], writes=[pF_tb])
            S.add("dve", lambda e: e.tensor_scalar(out=rstdc[:, :], in0=pF[:, 384:512], scalar1=1.0 / 1024.0, scalar2=EPS,
                                                   op0=ALU.mult, op1=ALU.add), reads=[pF_tb], writes=[rstdc_tb])
            S.add("act", lambda e: e.activation(out=rstdc[:, :], in_=rstdc[:, :], func=AF.Sqrt), reads=[rstdc_tb], writes=[rstdc_tb])
            S.add("dve", lambda e: e.reciprocal(out=rstdc[:, :], in_=rstdc[:, :]), reads=[rstdc_tb], writes=[rstdc_tb])
            for j in range(8):
                S.add("dve", lambda e, j=j: e.scalar_tensor_tensor(out=yT[:, 8 + j, :], in0=zt[:, j, :], scalar=cgt[:, j:j + 1],
                                                                   in1=rstdc[:, :], op0=ALU.mult, op1=ALU.mult),
                      reads=[MD_tb, cgt_tb, rstdc_tb], writes=[yTc_tb])
            for n in range(4):
                wv, wtb = wload(worow[:, n * 512:(n + 1) * 512].rearrange("(k p) c -> p k c", p=128), v16x512)
                for k in range(KC):
                    S.add("pe", lambda e, n=n, k=k, wv=wv: e.matmul(out=pbk(n), lhsT=yT[:, k, :], rhs=wv[:, k, :],
                                                                   start=(k == 0), stop=(k == KC - 1)),
                          reads=[yTg_tb, yTc_tb, wtb], writes=[pb_tb[n]])
            epilogue(l, i, dst_ap, dst_tbs[i])

    QF, QT, SC, KT, GG, SX, EX = 4, 5, 6, 7, 8, 9, 10
    def peer_layer(l, src_ap, src_tbs, dst_ap, dst_tbs):
        load_layer_common(l, 1)
        S.dma("sp", lambda e: e.dma_start(out=FB[KT][:, :], in_=keys_in[l]), "fb%d" % KT, writes=[FB_tb[KT]])
        keysT = FB[KT][:, :].rearrange("p (a n) -> p a n", a=16)
        qTv = FB[QT][:, :].rearrange("p (a n) -> p a n", a=16)
        scv = FB[SC][:, :].rearrange("p (a n) -> p a n", a=16)
        wkv = FB[QF][:, :].rearrange("p (a n) -> p a n", a=16)
        cand = FB[QT][:, :].rearrange("p (h c) -> p h c", h=8)
        wqrow = woq_f.ap()[l * 2 * D + D:(l + 1) * 2 * D, :]
        for i in range(NT):
            load_x_and_hT(src_ap, src_tbs[i], i, 3, 4)
            for n in range(4):
                wv, wtb = wload(wqrow[:, n * 512:(n + 1) * 512].rearrange("(k p) c -> p k c", p=128), v16x512)
                for k in range(KC):
                    S.add("pe", lambda e, n=n, k=k, wv=wv: e.matmul(out=pbk(n), lhsT=hT[:, k, :], rhs=wv[:, k, :],
                                                                   start=(k == 0), stop=(k == KC - 1)),
                          reads=[hT_tb, wtb], writes=[pb_tb[n]])
                S.add("act", lambda e, n=n: e.activation(out=FB[QF][:, n * 512:(n + 1) * 512], in_=pbk(n), func=AF.Copy),
                      reads=[pb_tb[n]], writes=[FB_tb[QF]])
            for a in range(16):
                S.add("pe", lambda e, a=a: e.transpose(out=pF[:, (a % 4) * 128:(a % 4 + 1) * 128],
                                                      in_=FB[QF][:, a * 128:(a + 1) * 128], identity=ident_f),
                      reads=[FB_tb[QF], cst_tb], writes=[pF_tb])
                S.add("act", lambda e, a=a: e.activation(out=FB[QT][:, a * 128:(a + 1) * 128], in_=pF[:, (a % 4) * 128:(a % 4 + 1) * 128],
                                                        func=AF.Copy), reads=[pF_tb], writes=[FB_tb[QT]])
            for a in range(16):
                S.add("pe", lambda e, a=a: e.matmul(out=pbig[:, a * 128:(a + 1) * 128], lhsT=qTv[:, a, :], rhs=keysT[:, a, :],
                                                   start=True, stop=True),
                      reads=[FB_tb[QT], FB_tb[KT]], writes=[pb_tb[a // 4]])
            for n in range(4):
                S.add("act", lambda e, n=n: e.activation(out=FB[SC][:, n * 512:(n + 1) * 512], in_=pbk(n), func=AF.Copy),
                      reads=[pb_tb[n]], writes=[FB_tb[SC]])
                S.add("act", lambda e, n=n: e.activation(out=FB[QF][:, n * 512:(n + 1) * 512], in_=pbk(n), func=AF.Copy),
                      reads=[pb_tb[n]], writes=[FB_tb[QF]])
            if PCUT <= 1:
                continue
            for a in range(16):
                S.add("dve", lambda e, a=a: e.max(out=svt[:, a, 0:8], in_=wkv[:, a, :]), reads=[FB_tb[QF]], writes=[sv_tb])
                S.add("dve", lambda e, a=a: e.match_replace(out=wkv[:, a, :], in_to_replace=svt[:, a, 0:8], in_values=wkv[:, a, :],
                                                            imm_value=-1e30), reads=[FB_tb[QF], sv_tb], writes=[FB_tb[QF]])
                S.add("dve", lambda e, a=a: e.max(out=svt[:, a, 8:16], in_=wkv[:, a, :]), reads=[FB_tb[QF]], writes=[sv_tb])
            for h in range(8):
                for ii in range(16):
                    S.add("dve", lambda e, h=h, ii=ii: e.tensor_scalar(out=cand[:, h, ii * 16:(ii + 1) * 16], in0=svt[:, 2 * h + 1, :],
                                                                        scalar1=svt[:, 2 * h, ii:ii + 1], scalar2=None, op0=ALU.add),
                          reads=[sv_tb], writes=[FB_tb[QT]])
            for h in range(8):
                S.add("dve", lambda e, h=h: e.max(out=cvv[:, h, 0:8], in_=cand[:, h, :]), reads=[FB_tb[QT]], writes=[cvv_tb])
                S.add("dve", lambda e, h=h: e.match_replace(out=cand[:, h, :], in_to_replace=cvv[:, h, 0:8], in_values=cand[:, h, :],
                                                            imm_value=-1e30), reads=[FB_tb[QT], cvv_tb], writes=[FB_tb[QT]])
                S.add("dve", lambda e, h=h: e.max(out=cvv[:, h, 8:16], in_=cand[:, h, :]), reads=[FB_tb[QT]], writes=[cvv_tb])
            dd = sm[:, 24:24 + 0]
            ddt = FB[SX][:, 0:128].rearrange("p (h k) -> p h k", h=8)
            S.add("dve", lambda e: e.tensor_scalar(out=sm[:, 48:56], in0=cvv[:, :, 0], scalar1=-1.0, scalar2=None, op0=ALU.mult),
                  reads=[cvv_tb], writes=[sm_tb])
            for h in range(8):
                S.add("act", lambda e, h=h: e.activation(out=ddt[:, h, :], in_=cvv[:, h, :], func=AF.Exp, bias=sm[:, 48 + h: 49 + h],
                                                        accum_out=sm[:, 32 + h: 33 + h]),
                      reads=[cvv_tb, sm_tb], writes=[FB_tb[SX], sm_tb])
            S.add("dve", lambda e: e.reciprocal(out=sm[:, 40:48], in_=sm[:, 32:40]), reads=[sm_tb], writes=[sm_tb])
            sc4 = scv.rearrange("p (h q) n -> p h q n", q=2)
            if PCUT <= 2:
                continue
            Gv = FB[GG][:, :]
            Sxv = FB[SX][:, :].rearrange("p (a n) -> p a n", a=16)
            Exv = FB[EX][:, :].rearrange("p (a n) -> p a n", a=16)
            for ch in range(8):
                for h in range(8):
                    for n1 in range(16):
                        S.add("dve", lambda e, h=h, ch=ch, n1=n1: e.tensor_scalar(
                            out=Sxv[:, n1, :], in0=sc4[:, h, 1, :], scalar1=sc4[:, h, 0, ch * 16 + n1: ch * 16 + n1 + 1], scalar2=None,
                            op0=ALU.add), reads=[FB_tb[SC]], writes=[FB_tb[SX]])
                    S.add("act", lambda e, h=h: e.activation(out=FB[EX][:, :], in_=FB[SX][:, :], func=AF.Exp, bias=sm[:, 48 + h: 49 + h]),
                          reads=[FB_tb[SX], sm_tb], writes=[FB_tb[EX]])
                    S.add("dve", lambda e, h=h: e.scalar_tensor_tensor(out=FB[EX][:, :], in0=FB[SX][:, :], scalar=cvv[:, h, 15:16], in1=FB[EX][:, :],
                                                                  op0=ALU.is_ge, op1=ALU.mult),
                          reads=[FB_tb[SX], FB_tb[EX], cvv_tb], writes=[FB_tb[EX]])
                    if h == 0:
                        S.add("dve", lambda e, h=h: e.tensor_scalar(out=Gv, in0=FB[EX][:, :], scalar1=sm[:, 40 + h: 41 + h], scalar2=None,
                                                                    op0=ALU.mult), reads=[FB_tb[EX], sm_tb], writes=[FB_tb[GG]])
                    else:
                        S.add("dve", lambda e, h=h: e.scalar_tensor_tensor(out=Gv, in0=FB[EX][:, :], scalar=sm[:, 40 + h: 41 + h], in1=Gv,
                                                                           op0=ALU.mult, op1=ALU.add),
                              reads=[FB_tb[EX], sm_tb, FB_tb[GG]], writes=[FB_tb[GG]])
                for g in range(4 if PCUT > 3 else 0):
                    gi = ch * 4 + g
                    pa = gi % 2
                    uv, utb = wload(ut_f.ap()[l * D * 8:(l + 1) * D * 8, :].rearrange("(d eb) c -> d eb c", eb=8)[:, gi // 4, (gi % 4) * 512:(gi % 4 + 1) * 512].rearrange("(k p) c -> p k c", p=128), v16x512)
                    for k in range(KC):
                        S.add("pe", lambda e, k=k, uv=uv, pa=pa: e.matmul(out=pA[pa][:, :], lhsT=hT[:, k, :], rhs=uv[:, k, :],
                                                                         start=(k == 0), stop=(k == KC - 1)),
                              reads=[hT_tb, utb], writes=[pA_tb[pa]])
                    tt = gt[pa][:, :]
                    ttb = gt_tb[pa]
                    S.add("act", lambda e, pa=pa, tt=tt: e.activation(out=tt, in_=pA[pa][:, :], func=AF.Square), reads=[pA_tb[pa]], writes=[ttb])
                    S.add("dve", lambda e, tt=tt: e.tensor_scalar(out=tt, in0=tt, scalar1=0.044715, scalar2=1.0, op0=ALU.mult, op1=ALU.add),
                          reads=[ttb], writes=[ttb])
                    S.add("dve", lambda e, pa=pa, tt=tt: e.tensor_tensor(out=tt, in0=tt, in1=pA[pa][:, :], op=ALU.mult),
                          reads=[ttb, pA_tb[pa]], writes=[ttb])
                    S.add("act", lambda e, tt=tt: e.activation(out=tt, in_=tt, func=AF.Sigmoid, scale=2.0 * C_GELU), reads=[ttb], writes=[ttb])
                    S.add("dve", lambda e, pa=pa, tt=tt: e.tensor_tensor(out=tt, in0=tt, in1=pA[pa][:, :], op=ALU.mult),
                          reads=[ttb, pA_tb[pa]], writes=[ttb])
                    S.add("dve", lambda e, pa=pa, tt=tt, g=g: e.tensor_tensor(out=Wb[pa][:, :], in0=tt, in1=Gv[:, g * 512:(g + 1) * 512], op=ALU.mult),
                          reads=[ttb, FB_tb[GG]], writes=[Wb_tb[pa]])
                    for j in range(4):
                        S.add("pe", lambda e, j=j, pa=pa: e.transpose(out=pT[:, pa * 512 + j * 128: pa * 512 + (j + 1) * 128],
                                                                     in_=Wb[pa][:, j * 128:(j + 1) * 128], identity=ident_b),
                              reads=[Wb_tb[pa], identb_tb], writes=[pT_tb[pa]])
                    S.add("act", lambda e, pa=pa: e.activation(out=WTt[pa][:, :, :],
                                                              in_=pT[:, pa * 512:(pa + 1) * 512].rearrange("p (j t) -> p j t", j=4), func=AF.Copy),
                          reads=[pT_tb[pa]], writes=[WT_tb[pa]])
                    vv, vtb = wload(v_f.ap()[l * NEXP + gi * 512: l * NEXP + (gi + 1) * 512, :].rearrange("(j p) c -> p j c", p=128), v4x2048)
                    for j in range(4):
                        for n in range(4):
                            S.add("pe", lambda e, j=j, n=n, vv=vv, pa=pa, gi=gi: e.matmul(
                                out=pbk(n), lhsT=WTt[pa][:, j, :], rhs=vv[:, j, n * 512:(n + 1) * 512],
                                start=(gi == 0 and j == 0), stop=(gi == 31 and j == 3)),
                                reads=[WT_tb[pa], vtb], writes=[pb_tb[n]])
            if PCUT <= 3:
                continue
            epilogue_peer(l, i, dst_ap, dst_tbs[i])

    def epilogue_peer(l, i, dst_ap, dst_tb):
        TA2, TB2 = 9, 10
        for n in range(4):
            S.add("dve", lambda e, n=n: e.tensor_tensor(out=FB[TA2][:, n * 512:(n + 1) * 512], in0=pbk(n),
                                                        in1=FB[G1B][:, n * 512:(n + 1) * 512], op=ALU.mult),
                  reads=[pb_tb[n], FB_tb[G1B]], writes=[FB_tb[TA2]])
        S.add("dve", lambda e: e.scalar_tensor_tensor(out=FB[TA2][:, :], in0=FB[XT][:, :], scalar=ALPHA, in1=FB[TA2][:, :],
                                                      op0=ALU.mult, op1=ALU.add),
              reads=[FB_tb[XT], FB_tb[TA2]], writes=[FB_tb[TA2]])
        for n in range(4):
            S.add("dve", lambda e, n=n: e.bn_stats(out=bnst[:, n, :], in_=FB[TA2][:, n * 512:(n + 1) * 512]),
                  reads=[FB_tb[TA2]], writes=[bnst_tb])
        S.add("dve", lambda e: e.bn_aggr(out=mv[:, :], in_=bnst[:, :, :].rearrange("p a b -> p (a b)")), reads=[bnst_tb], writes=[mv_tb])
        S.add("dve", lambda e: e.tensor_scalar(out=sm[:, 0:1], in0=mv[:, 1:2], scalar1=EPS, scalar2=None, op0=ALU.add),
              reads=[mv_tb], writes=[sm_tb])
        S.add("act", lambda e: e.activation(out=sm[:, 1:2], in_=sm[:, 0:1], func=AF.Sqrt), reads=[sm_tb], writes=[sm_tb])
        S.add("dve", lambda e: e.reciprocal(out=sm[:, 2:3], in_=sm[:, 1:2]), reads=[sm_tb], writes=[sm_tb])
        S.add("dve", lambda e: e.tensor_scalar(out=FB[TB2][:, :], in0=FB[TA2][:, :], scalar1=mv[:, 0:1], scalar2=sm[:, 2:3],
                                               op0=ALU.subtract, op1=ALU.mult),
              reads=[FB_tb[TA2], mv_tb, sm_tb], writes=[FB_tb[TB2]])
        S.add("pool", lambda e: e.tensor_tensor(out=FB[TB2][:, :], in0=FB[TB2][:, :], in1=FB[LGB][:, :], op=ALU.mult),
              reads=[FB_tb[TB2], FB_tb[LGB]], writes=[FB_tb[TB2]])
        S.add("pool", lambda e: e.tensor_tensor(out=FB[TB2][:, :], in0=FB[TB2][:, :], in1=FB[LBB][:, :], op=ALU.add),
              reads=[FB_tb[TB2], FB_tb[LBB]], writes=[FB_tb[TB2]])
        S.dma("sp", lambda e: e.dma_start(out=dst_ap[i * 128:(i + 1) * 128, :], in_=FB[TB2][:, :]), "fb%d" % TB2,
              reads=[FB_tb[TB2]], writes=[dst_tb])

    xin_tbs = [xin_tb] * NT
    out_tbs = [out_tb] * NT
    cur_ap, cur_tbs = x_in, xin_tbs
    for l in range(L if STAGE >= 2 else 0):
        if STAGE == 4:
            load_modcols(l)
            peer_layer(l, cur_ap, cur_tbs, out_d, out_tbs)
            continue
        mixer_layer(l, cur_ap, cur_tbs, xs[0].ap(), xs_tb[0])
        last = (l == L - 1)
        dst_ap = out_d if last else xs[1].ap()
        dst_tbs = out_tbs if last else xs_tb[1]
        if STAGE >= 3:
            peer_layer(l, xs[0].ap(), xs_tb[0], dst_ap, dst_tbs)
        cur_ap, cur_tbs = dst_ap, dst_tbs

    fin_waits = S._waits("sp", [out_tb], [out_tb])

    def fin(e):
        for (k, v) in fin_waits:
            e.wait_ge(S.sems[k], v)

    with nc.Block() as block:
        @block.gpsimd
        def _(g):
            emit_gathers(g)
            S.replay("pool", g)

        @block.sync
        def _(e):
            S.replay("sp", e)
            fin(e)

        @block.tensor
        def _(e):
            S.replay("pe", e)

        @block.scalar
        def _(e):
            S.replay("act", e)

        @block.vector
        def _(e):
            S.replay("dve", e)
    st.close()
    return nc


def _shard_rows(mat, ngroups):
    Lx, ROWS, W = mat.shape
    GR = ROWS // ngroups
    SR = GR // NCORES
    m = mat.reshape(Lx, ngroups, NCORES, SR, W)
    return [np.ascontiguousarray(m[:, :, r].reshape(Lx * ngroups * SR, W)) for r in range(NCORES)]


def make_consts():
    c = np.zeros((128, 640), np.float32)
    c[:, 0:128] = np.eye(128, dtype=np.float32)
    tp = np.arange(128)[:, None]
    t = np.arange(128)[None, :]
    c[:, 128:256] = np.where((tp // 64 == t // 64) & (tp > t), -1.0 / TAU, 0.0)
    c[:, 256:384] = 1.0
    c[:64, 384] = -1.0 / TAU
    c[64:, 385] = -1.0 / TAU
    c[:64, 386] = 1.0
    c[64:, 387] = 1.0
    return c


def prepare_inputs(inp, T, L):
    f = lambda a: np.ascontiguousarray(np.asarray(a, dtype=np.float32))
    x = f(inp["x"])[:, :T]
    c = f(inp["c"])
    adaw = _shard_rows(f(inp["ada_w"])[:L], 4)
    win = _shard_rows(f(inp["w_in"])[:L], 2)
    woq = _shard_rows(np.concatenate([f(inp["w_out"])[:L], f(inp["peer_wq"])[:L]], axis=1), 1)
    ut = _shard_rows(np.ascontiguousarray(f(inp["peer_u"])[:L].transpose(0, 2, 1)).reshape(L, D * 8, D), 4)
    vv = _shard_rows(f(inp["peer_v"])[:L], 4)
    wg = np.zeros((L, 32, 512), np.float32)
    wg[:, 0:16] = f(inp["w_gate2"])[:L]
    wg[:, 16] = f(inp["b_gate"])[:L]
    cw = f(inp["conv_w"])[:L]
    cwc = np.ascontiguousarray(cw.reshape(L, 3, 8, 128).transpose(0, 3, 2, 1).reshape(L, 128, 24))
    cg = np.ascontiguousarray(f(inp["conv_norm_g"])[:L].reshape(L, 8, 128).transpose(0, 2, 1))
    keys = f(inp["peer_keys"])[:L]
    keysT = np.ascontiguousarray(keys.reshape(L, 16, 128, 128).transpose(0, 3, 1, 2).reshape(L, 128, 2048))
    consts = make_consts()
    common = {
        "ada_b": f(inp["ada_b"])[:L], "wg": wg, "gg": f(inp["gla_norm_g"])[:L], "cw": cwc, "cg": cg,
        "ln1g": f(inp["ln1_g"])[:L], "ln1b": f(inp["ln1_b"])[:L], "ln2g": f(inp["ln2_g"])[:L], "ln2b": f(inp["ln2_b"])[:L],
        "keysT": keysT, "consts": consts,
    }
    maps = []
    for r in range(NCORES):
        m = dict(common)
        m["x"] = np.ascontiguousarray(x[r])
        m["c_col"] = np.ascontiguousarray(c[r].reshape(KC, 128).T)
        m["adaw_s"] = adaw[r]; m["win_s"] = win[r]; m["woq_s"] = woq[r]
        m["ut_s"] = ut[r]; m["v_s"] = vv[r]
        maps.append(m)
    return maps


def run(inp, T=2048, L=4):
    nc = build_program(T, L)
    maps = prepare_inputs(inp, T, L)
    res = run_bass_kernel_spmd(nc, maps, core_ids=list(range(NCORES)))
    if DEBUG:
        global DBG
        DBG = np.stack([np.asarray(res.results[r]["xs0"], dtype=np.float32) for r in range(NCORES)], axis=0)
    return np.stack([np.asarray(res.results[r]["out"], dtype=np.float32) for r in range(NCORES)], axis=0)


def kernel(**inputs):
    return run(inputs, 2048, 4)
```
